# Optimizing a Trainium2 kernel written in Bass

```python
import jax, jax.numpy as jnp
from jax import lax
import numpy as np

D_MODEL = 2048
BATCH = 2
SEQ = 8192
DEPTH = 2

MEM_LEN = 256
N_BRANCH = 4
BRANCH_W = 1024
POOL_GROUPS = 4
POOL_WINDOWS = (2, 4, 8, 16)
POOL_GW = BRANCH_W // POOL_GROUPS
MLA_HEADS = 8
Q_LORA = 512
KV_LORA = 512
QK_NOPE = 128
QK_ROPE = 64
QK_HEAD = QK_NOPE + QK_ROPE
V_HEAD = 128
ROPE_THETA = 10000.0
CONV_W = 3
XATTN_HEADS = 4
XATTN_HEAD_DIM = BRANCH_W // XATTN_HEADS
Q_BLOCK = 128
EPS = 1e-6
IN_SPLITS = (BRANCH_W, BRANCH_W, Q_LORA, KV_LORA, QK_ROPE, BRANCH_W, BRANCH_W, BRANCH_W, BRANCH_W, BRANCH_W, BRANCH_W, BRANCH_W, N_BRANCH * D_MODEL)
N_IN = 9 * BRANCH_W + Q_LORA + KV_LORA + QK_ROPE + N_BRANCH * D_MODEL

kernel_name = 'hybrid_gated_pool_mla_conv_memxattn'


def rms_norm(x, g):
    xf = x.astype(jnp.float32)
    y = xf * lax.rsqrt(jnp.mean(xf * xf, axis=-1, keepdims=True) + EPS)
    return (y * g.astype(jnp.float32)).astype(x.dtype)


def rope_tables(positions):
    inv = ROPE_THETA ** (-jnp.arange(0, QK_ROPE, 2, dtype=jnp.float32) / QK_ROPE)
    ang = positions.astype(jnp.float32)[..., None] * inv
    return jnp.cos(ang)[:, :, None, :], jnp.sin(ang)[:, :, None, :]


def rotate_tail(xh, cos, sin):
    nope = xh[..., :QK_NOPE]
    r = xh[..., QK_NOPE:].astype(jnp.float32)
    r1, r2 = r[..., :QK_ROPE // 2], r[..., QK_ROPE // 2:]
    rot = jnp.concatenate([r1 * cos - r2 * sin, r2 * cos + r1 * sin], axis=-1).astype(xh.dtype)
    return jnp.concatenate([nope, rot], axis=-1)


def pool_mixer(v, pool_w, pool_scale):
    B, S, _ = v.shape
    vg = v.reshape(B, S, POOL_GROUPS, POOL_GW).astype(jnp.float32)
    cs = jnp.cumsum(vg, axis=1)
    win = jnp.array(POOL_WINDOWS, dtype=jnp.int32)
    t = jnp.arange(S, dtype=jnp.int32)
    prev = t[:, None] - win[None, :]
    cs_prev = cs[:, jnp.maximum(prev, 0), jnp.arange(POOL_GROUPS)[None, :], :]
    cs_prev = jnp.where((prev >= 0)[None, :, :, None], cs_prev, 0.0)
    cnt = jnp.minimum(t[:, None] + 1, win[None, :]).astype(jnp.float32)
    mixed = ((cs - cs_prev) / cnt[None, :, :, None] - vg).astype(v.dtype)
    out = jnp.einsum('bsgc,gcd->bsgd', mixed, pool_w)
    return out.reshape(B, S, BRANCH_W) * pool_scale


def causal_block_attention(q, k, v):
    B, S, H, Dh = q.shape
    nb = S // Q_BLOCK
    scale = Dh ** -0.5
    qb = q.reshape(B, nb, Q_BLOCK, H, Dh).transpose(1, 0, 2, 3, 4)
    starts = jnp.arange(nb, dtype=jnp.int32) * Q_BLOCK
    kpos = jnp.arange(S, dtype=jnp.int32)

    def one_block(args):
        qi, s0 = args
        s = jnp.einsum('bqhd,bkhd->bhqk', qi, k).astype(jnp.float32) * scale
        mask = kpos[None, :] <= (s0 + jnp.arange(Q_BLOCK, dtype=jnp.int32))[:, None]
        s = jnp.where(mask[None, None], s, -jnp.inf)
        p = jax.nn.softmax(s, axis=-1).astype(v.dtype)
        return jnp.einsum('bhqk,bkhd->bqhd', p, v)

    o = lax.map(one_block, (qb, starts))
    return o.transpose(1, 0, 2, 3, 4).reshape(B, S, H, v.shape[-1])


def mla_mixer(cq, ckv, krope, cos, sin, q_a_g, kv_a_g, w_uq, w_ukv, q_g, k_g):
    B, S, _ = cq.shape
    q = (rms_norm(cq, q_a_g) @ w_uq).reshape(B, S, MLA_HEADS, QK_HEAD)
    kv = (rms_norm(ckv, kv_a_g) @ w_ukv).reshape(B, S, MLA_HEADS, QK_NOPE + V_HEAD)
    k_nope, v = kv[..., :QK_NOPE], kv[..., QK_NOPE:]
    k = jnp.concatenate([k_nope, jnp.broadcast_to(krope[:, :, None, :], (B, S, MLA_HEADS, QK_ROPE))], axis=-1)
    q = rotate_tail(rms_norm(q, q_g), cos, sin)
    k = rotate_tail(rms_norm(k, k_g), cos, sin)
    o = causal_block_attention(q, k, v)
    return o.reshape(B, S, MLA_HEADS * V_HEAD)


def conv_mixer(b, c, xc, conv_w):
    u = c * xc
    y = lax.conv_general_dilated(u, conv_w[:, None, :].astype(u.dtype), window_strides=(1,), padding=[(CONV_W - 1, 0)], dimension_numbers=('NWC', 'WIO', 'NWC'), feature_group_count=BRANCH_W)
    return b * y


def memory_xattn(q, mem_kv, q_g, k_g):
    B, S, _ = q.shape
    M = mem_kv.shape[1]
    qh = rms_norm(q.reshape(B, S, XATTN_HEADS, XATTN_HEAD_DIM), q_g)
    k = rms_norm(mem_kv[..., :BRANCH_W].reshape(B, M, XATTN_HEADS, XATTN_HEAD_DIM), k_g)
    v = mem_kv[..., BRANCH_W:].reshape(B, M, XATTN_HEADS, XATTN_HEAD_DIM)
    s = jnp.einsum('bshd,bmhd->bhsm', qh, k).astype(jnp.float32) * (XATTN_HEAD_DIM ** -0.5)
    p = jax.nn.softmax(s, axis=-1).astype(v.dtype)
    return jnp.einsum('bhsm,bmhd->bshd', p, v).reshape(B, S, BRANCH_W)


def setup_inputs(seed: int = 0) -> dict:
    key = jax.random.key(seed)
    ks = jax.random.split(key, 24)
    f32 = jnp.float32

    def nrm(k, shape, scale):
        return jax.random.normal(k, shape, f32) * scale

    def gain(k, shape):
        return 1.0 + 0.02 * jax.random.normal(k, shape, f32)

    offs = jax.random.randint(ks[2], (BATCH, 1), 0, 4096, dtype=jnp.int32)
    positions = offs + jnp.arange(SEQ, dtype=jnp.int32)[None, :]
    return {
        'x': nrm(ks[0], (BATCH, SEQ, D_MODEL), 1.0),
        'mem': nrm(ks[1], (BATCH, MEM_LEN, D_MODEL), 1.0),
        'positions': positions,
        'norm_g': gain(ks[3], (DEPTH, D_MODEL)),
        'w_in': nrm(ks[4], (DEPTH, D_MODEL, N_IN), D_MODEL ** -0.5),
        'gate_b': nrm(ks[5], (DEPTH, N_BRANCH * D_MODEL), 0.02),
        'pool_w': nrm(ks[6], (DEPTH, POOL_GROUPS, POOL_GW, POOL_GW), POOL_GW ** -0.5),
        'pool_scale': gain(ks[7], (DEPTH, BRANCH_W)),
        'q_a_norm_g': gain(ks[8], (DEPTH, Q_LORA)),
        'kv_a_norm_g': gain(ks[9], (DEPTH, KV_LORA)),
        'w_uq': nrm(ks[10], (DEPTH, Q_LORA, MLA_HEADS * QK_HEAD), Q_LORA ** -0.5),
        'w_ukv': nrm(ks[11], (DEPTH, KV_LORA, MLA_HEADS * (QK_NOPE + V_HEAD)), KV_LORA ** -0.5),
        'mla_q_norm_g': gain(ks[12], (DEPTH, QK_HEAD)),
        'mla_k_norm_g': gain(ks[13], (DEPTH, QK_HEAD)),
        'conv_w': nrm(ks[14], (DEPTH, CONV_W, BRANCH_W), CONV_W ** -0.5),
        'mem_norm_g': gain(ks[15], (DEPTH, D_MODEL)),
        'w_mem_kv': nrm(ks[16], (DEPTH, D_MODEL, 2 * BRANCH_W), D_MODEL ** -0.5),
        'xattn_q_norm_g': gain(ks[17], (DEPTH, XATTN_HEAD_DIM)),
        'xattn_k_norm_g': gain(ks[18], (DEPTH, XATTN_HEAD_DIM)),
        'w_branch': nrm(ks[19], (DEPTH, N_BRANCH, BRANCH_W, D_MODEL), BRANCH_W ** -0.5),
        'w_out': nrm(ks[20], (DEPTH, D_MODEL, D_MODEL), D_MODEL ** -0.5),
    }


def reference(x, mem, positions, norm_g, w_in, gate_b, pool_w, pool_scale, q_a_norm_g, kv_a_norm_g, w_uq, w_ukv, mla_q_norm_g, mla_k_norm_g, conv_w, mem_norm_g, w_mem_kv, xattn_q_norm_g, xattn_k_norm_g, w_branch, w_out):
    B, S, _ = x.shape
    cos, sin = rope_tables(positions)
    split_points = np.cumsum(IN_SPLITS)[:-1].tolist()
    for l in range(DEPTH):
        h = rms_norm(x, norm_g[l])
        proj = h @ w_in[l]
        (pv, pz, cq, ckv, kr, mz, cb, cc, cx, cz, xq, xz, gpre) = jnp.split(proj, split_points, axis=-1)
        y_pool = pool_mixer(pv, pool_w[l], pool_scale[l]) * jax.nn.silu(pz)
        y_mla = mla_mixer(cq, ckv, kr, cos, sin, q_a_norm_g[l], kv_a_norm_g[l], w_uq[l], w_ukv[l], mla_q_norm_g[l], mla_k_norm_g[l]) * jax.nn.silu(mz)
        y_conv = conv_mixer(cb, cc, cx, conv_w[l]) * jax.nn.silu(cz)
        mem_kv = rms_norm(mem, mem_norm_g[l]) @ w_mem_kv[l]
        y_mem = memory_xattn(xq, mem_kv, xattn_q_norm_g[l], xattn_k_norm_g[l]) * jax.nn.silu(xz)
        gates = jax.nn.sigmoid((gpre + gate_b[l]).astype(jnp.float32)).astype(x.dtype).reshape(B, S, N_BRANCH, D_MODEL)
        merged = gates[:, :, 0] * (y_pool @ w_branch[l, 0])
        merged = merged + gates[:, :, 1] * (y_mla @ w_branch[l, 1])
        merged = merged + gates[:, :, 2] * (y_conv @ w_branch[l, 2])
        merged = merged + gates[:, :, 3] * (y_mem @ w_branch[l, 3])
        x = x + merged @ w_out[l]
    return x
```

```python
import math
import numpy as np
from contextlib import ExitStack
import concourse.bass as bass
import concourse.mybir as mybir
from concourse.bass_utils import run_bass_kernel_spmd

F32 = mybir.dt.float32
I32 = mybir.dt.int32
AF = mybir.ActivationFunctionType
ALU = mybir.AluOpType
D = 2048
NIN = 18496
C = 256
H = 16
W = C + H
NCH = 8
EPS = 1e-6
OFF = dict(pv=0, pz=1024, cq=2048, ckv=2560, kr=3072, mz=3136, cb=4160, cc=5184, cx=6208,
           cz=7232, xq=8256, xz=9280, g=10304)
KB = 8


class Prog:
    def __init__(self):
        self.ops = []

    def op(self, eng, f):
        self.ops.append((eng, f))

    def mm(self, out, lhsT, rhs, start=True, stop=True):
        self.op('tensor', lambda e: e.matmul(out, lhsT, rhs, start=start, stop=stop))

    def dma(self, out, in_):
        self.op('sync', lambda e: e.dma_start(out=out, in_=in_))

    def tt(self, out, a, b, op):
        self.op('vector', lambda e: e.tensor_tensor(out, a, b, op))

    def ts(self, out, a, s1, s2, op0, op1=None):
        if op1 is None:
            self.op('vector', lambda e: e.tensor_scalar(out, a, s1, None, op0))
        else:
            self.op('vector', lambda e: e.tensor_scalar(out, a, s1, s2, op0, op1))

    def stt(self, out, a, s, b, op0, op1):
        self.op('vector', lambda e: e.scalar_tensor_tensor(out, a, s, b, op0, op1))

    def copy(self, out, a):
        self.op('vector', lambda e: e.tensor_copy(out, a))

    def recip(self, out, a):
        self.op('vector', lambda e: e.reciprocal(out, a))

    def act(self, out, a, func, bias=None, scale=None):
        kw = {}
        if bias is not None:
            kw['bias'] = bias
        if scale is not None:
            kw['scale'] = scale
        self.op('scalar', lambda e: e.activation(out, a, func, **kw))

    def emit(self, nc, stack):
        engs = ['tensor', 'vector', 'scalar', 'sync']
        incs = {'sync': 16}
        counts = {e: 0 for e in engs}
        plan = []
        prev = None
        for (e, f) in self.ops:
            w = None
            if prev is not None and not (prev == 'tensor' and e == 'tensor'):
                w = (prev, counts[prev])
            counts[e] += incs.get(e, 1)
            plan.append((e, f, w))
            prev = e
        sems = {e: stack.enter_context(nc.semaphore('s_' + e)) for e in engs}
        last = (prev, counts[prev])
        with nc.Block() as block:
            for e in engs:
                def body(eng, e=e):
                    for (oe, f, w) in plan:
                        if oe != e:
                            continue
                        if w is not None:
                            eng.wait_ge(sems[w[0]], w[1])
                        f(eng).then_inc(sems[e], incs.get(e, 1))
                    if e == 'sync':
                        eng.wait_ge(sems[last[0]], last[1])
                getattr(block, e)(body)


def build(mode):
    nc = bass.Bass("TRN2", target_bir_lowering=False)
    P = Prog()

    def din(name, shape, dt=F32):
        return nc.dram_tensor(name, list(shape), dt, kind="ExternalInput").ap()

    def dout(name, shape):
        return nc.dram_tensor(name, list(shape), F32, kind="ExternalOutput").ap()

    xh = din("xh", [NCH, D, W])
    w_in = din("w_in", [D, NIN])
    norm_g_d = din("norm_g", [128, 16])
    ones_d = din("ones", [128, 128])
    rrot_d = din("rrot", [64, 64])
    inv_d = din("inv64", [64, 1])
    pos_d = din("pos", [64, NCH, C], I32)
    if mode == 'A':
        kvg_d = din("kv_a_g", [128, 4])
        w_ukv = din("w_ukv", [512, 2048])
        kgn_d = din("kg_n", [128, 1])
        kgr_d = din("kg_r", [64, 1])
        kout = dout("kout", [NCH, 8, 192, C])
        vout = dout("vout", [NCH, C, 1024])
    else:
        gate_b_d = din("gate_b", [128, 64])
        pool_w = din("pool_w", [4, 256, 256])
        pool_scale_d = din("pool_scale", [128, 8])
        qag_d = din("q_a_g", [128, 4])
        w_uq = din("w_uq", [512, 1536])
        qgn_d = din("qg_n", [128, 1])
        qgr_d = din("qg_r", [64, 1])
        conv_w_d = din("conv_w", [128, 3, 8])
        memT_d = din("memT", [D, 256])
        mem_g_d = din("mem_g", [128, 16])
        w_mem_kv = din("w_mem_kv", [D, 2048])
        xqg_d = din("xq_g", [128, 2])
        xkg_d = din("xk_g", [128, 2])
        w_branch = din("w_branch", [4, 1024, D])
        w_out = din("w_out", [D, D])
        KT = din("KT", [8, 192, 8192])
        Vd = din("V", [8192, 1024])
        masks_d = din("masks", [8, 128, C])
        rcnt_d = din("rcnt", [2, 128, 4, C])
        out_d = dout("out", [NCH, D, C])

    with ExitStack() as stack:
        def sb(name, shape, dt=F32):
            return stack.enter_context(nc.sbuf_tensor("sb_" + name, list(shape), dt))

        def ps(name):
            return stack.enter_context(nc.psum_tensor(name, [128, 512], F32))

        xT = sb("xT", [128, 16, W])
        sq = sb("sq", [128, 16, W])
        hT = sb("hT", [128, 16, W])
        wt = sb("wt", [128, 16, 128])
        rs = sb("rs", [128, W])
        sqt = sb("sqt", [128, W])
        g_sb = sb("g_sb", [128, 16])
        ones = sb("ones_sb", [128, 128])
        rrot = sb("rrot_sb", [64, 64])
        inv64 = sb("inv_sb", [64, 1])
        posi = sb("posi", [64, NCH, C], I32)
        posf = sb("posf", [64, C])
        ang = sb("ang", [64, C])
        cos64 = sb("cos64", [64, C])
        sin64 = sb("sin64", [64, C])
        t64 = sb("t64", [64, C])
        pp = ps("pp")
        ss = ps("ss")
        rr = ps("rr")

        P.dma(g_sb[:, :], norm_g_d[:, :])
        P.dma(ones[:, :], ones_d[:, :])
        P.dma(rrot[:, :], rrot_d[:, :])
        P.dma(inv64[:, :], inv_d[:, :])
        P.dma(posi[:, :, :], pos_d[:, :, :])

        def rstd(parts, n, inv_count, dst):
            for i, (ap, npart) in enumerate(parts):
                P.tt(sqt[:npart, :n], ap, ap, ALU.mult)
                P.mm(ss[:, :n], ones[:npart, :], sqt[:npart, :n], start=(i == 0), stop=(i == len(parts) - 1))
            P.ts(dst[:, :n], ss[:, :n], inv_count, EPS, ALU.mult, ALU.add)
            P.act(dst[:, :n], dst[:, :n], AF.Sqrt)
            P.recip(dst[:, :n], dst[:, :n])

        def load_w(dst, src2d, nt, width):
            v = src2d.rearrange("(t p) n -> p t n", p=128)
            step = 4
            for t0 in range(0, nt, step):
                t1 = min(nt, t0 + step)
                P.dma(dst[:, t0:t1, :width], v[:, t0:t1, :])

        def proj(off, width, halo):
            c0 = 0 if halo else H
            n = W - c0
            load_w(wt, w_in[:, off:off + width], 16, width)
            for dt in range(16):
                P.mm(pp[:width, :n], wt[:, dt, :width], hT[:, dt, c0:W], start=(dt == 0), stop=(dt == 15))
            return pp[:width, :n]

        def norm_x(j):
            P.dma(xT[:, 0:8, :], xh[j, 0:1024, :].rearrange("(t p) w -> p t w", p=128))
            P.dma(xT[:, 8:16, :], xh[j, 1024:2048, :].rearrange("(t p) w -> p t w", p=128))
            rstd([(xT[:, dt, :], 128) for dt in range(16)], W, 1.0 / D, rs)
            for dt in range(16):
                P.stt(hT[:, dt, :], xT[:, dt, :], g_sb[:, dt:dt + 1], rs[:, :], ALU.mult, ALU.mult)

        ki64 = sb("ki64", [64, C], I32)

        def sinseq(dst, shift):
            t = t64[:, :]
            P.ts(t, ang[:, :], shift, 1.0 / (2 * math.pi), ALU.add, ALU.mult)
            P.copy(ki64[:, :], t)
            P.copy(t, ki64[:, :])
            P.stt(t, t, -2 * math.pi, ang[:, :], ALU.mult, ALU.add)
            P.ts(t, t, shift, None, ALU.add)
            P.ts(dst, t, math.pi, 2 * math.pi, ALU.is_gt, ALU.mult)
            P.tt(t, t, dst, ALU.subtract)
            P.ts(dst, t, -math.pi, 2 * math.pi, ALU.is_lt, ALU.mult)
            P.tt(t, t, dst, ALU.add)
            P.act(dst, t, AF.Sin)

        def rope_tables(j):
            P.copy(posf[:, :], posi[:, j, :])
            P.ts(ang[:, :], posf[:, :], inv64[:, 0:1], None, ALU.mult)
            sinseq(sin64[:, :], 0.0)
            sinseq(cos64[:, :], math.pi / 2)

        def rope(dst, src):
            P.mm(rr[:64, :C], rrot[:, :], src, start=True, stop=True)
            P.tt(t64[:, :], rr[:64, :C], sin64[:, :], ALU.mult)
            P.tt(dst, src, cos64[:, :], ALU.mult)
            P.tt(dst, dst, t64[:, :], ALU.add)

        if mode == 'A':
            kvg = sb("kvg", [128, 4])
            kgn = sb("kgn", [128, 1])
            kgr = sb("kgr", [64, 1])
            wv = sb("wv", [128, 4, 8, 128])
            wk = sb("wk", [128, 4, 128])
            ckv = sb("ckv", [128, 4, C])
            krT = sb("krT", [64, C])
            krh = sb("krh", [64, C])
            kro = sb("kro", [64, C])
            kn = sb("kn", [128, C])
            rs2 = sb("rs2", [128, C])
            vsb = sb("vsb", [128, 512])
            P.dma(kvg[:, :], kvg_d[:, :])
            P.dma(kgn[:, :], kgn_d[:, :])
            P.dma(kgr[:, :], kgr_d[:, :])
            wv_src = w_ukv.rearrange("(ct p) (h two d) -> p ct h two d", p=128, two=2, d=128)
            for ct in range(4):
                P.dma(wv[:, ct, :, :], wv_src[:, ct, :, 1, :])
            for j in range(NCH):
                norm_x(j)
                rope_tables(j)
                for i in range(4):
                    o = proj(OFF['ckv'] + i * 128, 128, False)
                    P.copy(ckv[:, i, :], o)
                o = proj(OFF['kr'], 64, False)
                P.copy(krT[:, :], o)
                rstd([(ckv[:, i, :], 128) for i in range(4)], C, 1.0 / 512, rs2)
                for i in range(4):
                    P.stt(ckv[:, i, :], ckv[:, i, :], kvg[:, i:i + 1], rs2[:, :], ALU.mult, ALU.mult)
                for tt_ in range(2):
                    for nb in range(2):
                        for ct in range(4):
                            P.mm(pp[:, :512], ckv[:, ct, tt_ * 128:(tt_ + 1) * 128],
                                 wv[:, ct, nb * 4:(nb + 1) * 4, :], start=(ct == 0), stop=(ct == 3))
                        P.copy(vsb[:, :], pp[:, :512])
                        P.dma(vout[j, tt_ * 128:(tt_ + 1) * 128, nb * 512:(nb + 1) * 512], vsb[:, :])
                for hh in range(8):
                    load_w(wk, w_ukv[:, hh * 256:hh * 256 + 128], 4, 128)
                    for ct in range(4):
                        P.mm(pp[:, :C], wk[:, ct, :], ckv[:, ct, :], start=(ct == 0), stop=(ct == 3))
                    P.copy(kn[:, :], pp[:, :C])
                    rstd([(kn[:, :], 128), (krT[:, :], 64)], C, 1.0 / 192, rs2)
                    P.stt(kn[:, :], kn[:, :], kgn[:, 0:1], rs2[:, :], ALU.mult, ALU.mult)
                    P.stt(krh[:, :], krT[:, :], kgr[:, 0:1], rs2[:64, :], ALU.mult, ALU.mult)
                    rope(kro[:, :], krh[:, :])
                    P.dma(kout[j, hh, 0:128, :], kn[:, :])
                    P.dma(kout[j, hh, 128:192, :], kro[:, :])
        else:
            gate_b = sb("gate_b", [128, 64])
            pool_scale = sb("pool_scale", [128, 8])
            qag = sb("qag", [128, 4])
            qgn = sb("qgn", [128, 1])
            qgr = sb("qgr", [64, 1])
            conv_w = sb("conv_w", [128, 3, 8])
            xqg = sb("xqg", [128, 2])
            xkg = sb("xkg", [128, 2])
            mem_g = sb("mem_g", [128, 16])
            masks = sb("masks", [128, 8, C])
            rcnt = sb("rcnt", [128, 2, 4, C])
            yb = sb("yb", [128, 4, 8, C])
            tA = sb("tA", [128, 8, W])
            tB = sb("tB", [128, 8, C])
            pa = sb("pa", [128, W])
            pb_ = sb("pb", [128, W])
            pc = sb("pc", [128, W])
            wq = sb("wq", [128, 4, 192])
            wsm = sb("wsm", [128, 8, 128])
            pw = sb("pw", [128, 2, 256])
            kmT = sb("kmT", [128, 8, 256])
            vm = sb("vm", [128, 2, 1024])
            wmv = sb("wmv", [128, 16, 256])
            knb = sb("knb", [128, KB * 128])
            krb = sb("krb", [64, KB * 128])
            vb = sb("vb", [128, KB, 128])
            qn = sb("qn", [128, C])
            qr = sb("qr", [64, C])
            qro = sb("qro", [64, C])
            pT = sb("pT", [128, C])
            pT2 = sb("pT2", [128, C])
            rs2 = sb("rs2", [128, C])
            osb = sb("osb", [128, C])
            sp = ps("sp")
            op_ = ps("op")
            dp = ps("dp")
            merged = sq

            for dst, src in [(gate_b, gate_b_d), (pool_scale, pool_scale_d), (qag, qag_d), (qgn, qgn_d),
                             (qgr, qgr_d), (xqg, xqg_d), (xkg, xkg_d), (mem_g, mem_g_d)]:
                P.dma(dst[:, :], src[:, :])
            P.dma(conv_w[:, :, :], conv_w_d[:, :, :])
            for i in range(8):
                P.dma(masks[:, i, :], masks_d[i, :, :])
            for i in range(2):
                P.dma(rcnt[:, i, :, :], rcnt_d[i, :, :, :])

            P.dma(xT[:, 0:8, :256], memT_d[0:1024, :].rearrange("(t p) w -> p t w", p=128))
            P.dma(xT[:, 8:16, :256], memT_d[1024:2048, :].rearrange("(t p) w -> p t w", p=128))
            rstd([(xT[:, dt, :256], 128) for dt in range(16)], 256, 1.0 / D, rs)
            for dt in range(16):
                P.stt(hT[:, dt, :256], xT[:, dt, :256], mem_g[:, dt:dt + 1], rs[:, :256], ALU.mult, ALU.mult)
            for i in range(8):
                load_w(wt, w_mem_kv[:, i * 128:(i + 1) * 128], 16, 128)
                for dt in range(16):
                    P.mm(pp[:, :256], wt[:, dt, :], hT[:, dt, :256], start=(dt == 0), stop=(dt == 15))
                P.copy(kmT[:, i, :], pp[:, :256])
            for hh in range(4):
                rstd([(kmT[:, 2 * hh, :], 128), (kmT[:, 2 * hh + 1, :], 128)], 256, 1.0 / 256, rs2)
                for e_ in range(2):
                    P.stt(kmT[:, 2 * hh + e_, :], kmT[:, 2 * hh + e_, :], xkg[:, e_:e_ + 1], rs2[:, :],
                          ALU.mult, ALU.mult)
            for nb in range(4):
                load_w(wmv, w_mem_kv[:, 1024 + nb * 256:1024 + (nb + 1) * 256], 16, 256)
                for mt in range(2):
                    for dt in range(16):
                        P.mm(pp[:, :256], hT[:, dt, mt * 128:(mt + 1) * 128], wmv[:, dt, :],
                             start=(dt == 0), stop=(dt == 15))
                    P.copy(vm[:, mt, nb * 256:(nb + 1) * 256], pp[:, :256])

            for j in range(NCH):
                norm_x(j)
                rope_tables(j)
                for b, zname in enumerate(['pz', 'mz', 'cz', 'xz']):
                    for i in range(8):
                        o = proj(OFF[zname] + i * 128, 128, False)
                        P.act(yb[:, b, i, :], o, AF.Silu)

                for i in range(8):
                    g = i // 2
                    o = proj(OFF['pv'] + i * 128, 128, True)
                    P.copy(tA[:, i, :], o)
                    src = tA[:, i, :]
                    bufs = [pa, pb_]
                    for k in range(g + 1):
                        s = 1 << k
                        dstb = bufs[k % 2]
                        P.tt(dstb[:, s:W], src[:, s:W], src[:, 0:W - s], ALU.add)
                        src = dstb[:, :]
                    ri = 0 if j == 0 else 1
                    P.tt(pc[:, :C], src[:, H:W], rcnt[:, ri, g, :], ALU.mult)
                    P.tt(tB[:, i, :], pc[:, :C], tA[:, i, H:W], ALU.subtract)
                for g in range(4):
                    P.dma(pw[:, :, :], pool_w[g, :, :].rearrange("(ct p) d -> p ct d", p=128))
                    for e_ in range(2):
                        for ct in range(2):
                            P.mm(pp[:, :C], pw[:, ct, e_ * 128:(e_ + 1) * 128], tB[:, 2 * g + ct, :],
                                 start=(ct == 0), stop=(ct == 1))
                        i = 2 * g + e_
                        P.stt(yb[:, 0, i, :], pp[:, :C], pool_scale[:, i:i + 1], yb[:, 0, i, :],
                              ALU.mult, ALU.mult)

                for i in range(8):
                    o = proj(OFF['cc'] + i * 128, 128, True)
                    P.copy(pa[:, :], o)
                    o = proj(OFF['cx'] + i * 128, 128, True)
                    P.tt(pb_[:, :], pa[:, :], o, ALU.mult)
                    P.ts(pc[:, :C], pb_[:, H - 2:W - 2], conv_w[:, 0, i:i + 1], None, ALU.mult)
                    P.stt(pc[:, :C], pb_[:, H - 1:W - 1], conv_w[:, 1, i:i + 1], pc[:, :C], ALU.mult, ALU.add)
                    P.stt(pc[:, :C], pb_[:, H:W], conv_w[:, 2, i:i + 1], pc[:, :C], ALU.mult, ALU.add)
                    o = proj(OFF['cb'] + i * 128, 128, False)
                    P.tt(pc[:, :C], pc[:, :C], o, ALU.mult)
                    P.tt(yb[:, 2, i, :], yb[:, 2, i, :], pc[:, :C], ALU.mult)

                for i in range(8):
                    o = proj(OFF['xq'] + i * 128, 128, False)
                    P.copy(tB[:, i, :], o)
                for hh in range(4):
                    rstd([(tB[:, 2 * hh, :], 128), (tB[:, 2 * hh + 1, :], 128)], C, 1.0 / 256, rs2)
                    for e_ in range(2):
                        P.stt(tB[:, 2 * hh + e_, :], tB[:, 2 * hh + e_, :], xqg[:, e_:e_ + 1], rs2[:, :],
                              ALU.mult, ALU.mult)
                    pts = [pT, pT2]
                    for mt in range(2):
                        for ct in range(2):
                            P.mm(sp[:, :C], kmT[:, 2 * hh + ct, mt * 128:(mt + 1) * 128], tB[:, 2 * hh + ct, :],
                                 start=(ct == 0), stop=(ct == 1))
                        P.act(pts[mt][:, :], sp[:, :C], AF.Exp, scale=1.0 / 16.0)
                    for mt in range(2):
                        P.mm(dp[:, :C], ones[:, :], pts[mt][:, :], start=(mt == 0), stop=(mt == 1))
                    P.recip(rs2[:, :], dp[:, :C])
                    for e_ in range(2):
                        for mt in range(2):
                            P.mm(op_[:, :C], vm[:, mt, hh * 256 + e_ * 128:hh * 256 + (e_ + 1) * 128],
                                 pts[mt][:, :], start=(mt == 0), stop=(mt == 1))
                        P.tt(osb[:, :], op_[:, :C], rs2[:, :], ALU.mult)
                        P.tt(yb[:, 3, 2 * hh + e_, :], yb[:, 3, 2 * hh + e_, :], osb[:, :], ALU.mult)

                for i in range(4):
                    o = proj(OFF['cq'] + i * 128, 128, False)
                    P.copy(tA[:, i, :C], o)
                rstd([(tA[:, i, :C], 128) for i in range(4)], C, 1.0 / 512, rs2)
                for i in range(4):
                    P.stt(tA[:, i, :C], tA[:, i, :C], qag[:, i:i + 1], rs2[:, :], ALU.mult, ALU.mult)
                nkt = 8 * j + 8
                for hh in range(8):
                    P.dma(wq[:, :, :], w_uq[:, hh * 192:(hh + 1) * 192].rearrange("(t p) n -> p t n", p=128))
                    for ct in range(4):
                        P.mm(pp[:, :C], wq[:, ct, 0:128], tA[:, ct, :C], start=(ct == 0), stop=(ct == 3))
                    P.copy(qn[:, :], pp[:, :C])
                    for ct in range(4):
                        P.mm(pp[:64, :C], wq[:, ct, 128:192], tA[:, ct, :C], start=(ct == 0), stop=(ct == 3))
                    P.copy(qr[:, :], pp[:64, :C])
                    rstd([(qn[:, :], 128), (qr[:, :], 64)], C, 1.0 / 192, rs2)
                    P.stt(qn[:, :], qn[:, :], qgn[:, 0:1], rs2[:, :], ALU.mult, ALU.mult)
                    P.stt(qr[:, :], qr[:, :], qgr[:, 0:1], rs2[:64, :], ALU.mult, ALU.mult)
                    rope(qro[:, :], qr[:, :])
                    for kb0 in range(0, nkt, KB):
                        P.dma(knb[:, :], KT[hh, 0:128, kb0 * 128:(kb0 + KB) * 128])
                        P.dma(krb[:, :], KT[hh, 128:192, kb0 * 128:(kb0 + KB) * 128])
                        P.dma(vb[:, :, :], Vd[kb0 * 128:(kb0 + KB) * 128, hh * 128:(hh + 1) * 128]
                              .rearrange("(t p) d -> p t d", p=128))
                        for kk in range(KB):
                            kt = kb0 + kk
                            P.mm(sp[:, :C], knb[:, kk * 128:(kk + 1) * 128], qn[:, :], start=True, stop=False)
                            P.mm(sp[:, :C], krb[:, kk * 128:(kk + 1) * 128], qro[:, :], start=False, stop=True)
                            P.act(pT[:, :], sp[:, :C], AF.Exp, scale=192.0 ** -0.5)
                            if kt >= 8 * j:
                                P.tt(pT[:, :], pT[:, :], masks[:, kt - 8 * j, :], ALU.mult)
                            P.mm(op_[:, :C], vb[:, kk, :], pT[:, :], start=(kt == 0), stop=(kt == nkt - 1))
                            P.mm(dp[:, :C], ones[:, :], pT[:, :], start=(kt == 0), stop=(kt == nkt - 1))
                    P.recip(rs2[:, :], dp[:, :C])
                    P.tt(osb[:, :], op_[:, :C], rs2[:, :], ALU.mult)
                    P.tt(yb[:, 1, hh, :], yb[:, 1, hh, :], osb[:, :], ALU.mult)

                for e_ in range(16):
                    for b in range(4):
                        o = proj(OFF['g'] + b * 2048 + e_ * 128, 128, False)
                        P.act(pT[:, :], o, AF.Sigmoid, bias=gate_b[:, b * 16 + e_:b * 16 + e_ + 1])
                        load_w(wsm, w_branch[b, :, e_ * 128:(e_ + 1) * 128], 8, 128)
                        for kt in range(8):
                            P.mm(sp[:, :C], wsm[:, kt, :], yb[:, b, kt, :], start=(kt == 0), stop=(kt == 7))
                        if b == 0:
                            P.tt(merged[:, e_, :C], pT[:, :], sp[:, :C], ALU.mult)
                        else:
                            P.tt(pT2[:, :], pT[:, :], sp[:, :C], ALU.mult)
                            P.tt(merged[:, e_, :C], merged[:, e_, :C], pT2[:, :], ALU.add)
                for e_ in range(16):
                    load_w(wt, w_out[:, e_ * 128:(e_ + 1) * 128], 16, 128)
                    for dt in range(16):
                        P.mm(pp[:, :C], wt[:, dt, :], merged[:, dt, :C], start=(dt == 0), stop=(dt == 15))
                    P.tt(osb[:, :], pp[:, :C], xT[:, e_, H:W], ALU.add)
                    P.dma(out_d[j, e_ * 128:(e_ + 1) * 128, :], osb[:, :])

        P.emit(nc, stack)
    return nc


_NC = {}


def _get_nc(mode):
    if mode not in _NC:
        _NC[mode] = build(mode)
    return _NC[mode]


def _col(v, k):
    return np.ascontiguousarray(np.asarray(v, np.float32).reshape(k, 128).T)


def kernel(x, mem, positions, norm_g, w_in, gate_b, pool_w, pool_scale, q_a_norm_g, kv_a_norm_g, w_uq, w_ukv,
           mla_q_norm_g, mla_k_norm_g, conv_w, mem_norm_g, w_mem_kv, xattn_q_norm_g, xattn_k_norm_g,
           w_branch, w_out):
    x = np.asarray(x, np.float32)
    mem = np.asarray(mem, np.float32)
    positions = np.asarray(positions, np.int32)
    f = lambda a: np.ascontiguousarray(np.asarray(a, np.float32))
    ones = np.ones((128, 128), np.float32)
    rrot = np.zeros((64, 64), np.float32)
    for m in range(32):
        rrot[m + 32, m] = -1.0
        rrot[m, m + 32] = 1.0
    inv = (np.float32(10000.0) ** (-np.arange(0, 64, 2, dtype=np.float32) / np.float32(64))).astype(np.float32)
    inv64 = np.concatenate([inv, inv]).reshape(64, 1).astype(np.float32)
    kp = np.arange(128)[:, None]
    qf = np.arange(C)[None, :]
    masks_r = []
    for r in range(4):
        masks_r.append(np.stack([((128 * i + kp) <= (256 * r + qf)).astype(np.float32) for i in range(8)]))
    wins = [2, 4, 8, 16]
    rc_first = np.zeros((128, 4, C), np.float32)
    rc_rest = np.zeros((128, 4, C), np.float32)
    for g, w_ in enumerate(wins):
        rc_first[:, g, :] = 1.0 / np.minimum(np.arange(C) + 1, w_).astype(np.float32)
        rc_rest[:, g, :] = 1.0 / np.float32(w_)

    def chunks_T(xcur):
        res = []
        for c in range(8):
            b, r = c // 4, c % 4
            xp = np.concatenate([np.zeros((H, D), np.float32), xcur[b]], axis=0)
            arr = np.empty((NCH, D, W), np.float32)
            for j in range(NCH):
                g = 4 * j + r
                arr[j] = xp[g * C:g * C + W].T
            res.append(arr)
        return res

    pos_c = []
    for c in range(8):
        b, r = c // 4, c % 4
        p = np.empty((NCH, C), np.int32)
        for j in range(NCH):
            g = 4 * j + r
            p[j] = positions[b, g * C:(g + 1) * C]
        pos_c.append(np.ascontiguousarray(np.broadcast_to(p[None], (64, NCH, C))))

    xcur = x
    for l in range(2):
        xhs = chunks_T(xcur)
        common = dict(w_in=f(w_in[l]), norm_g=_col(norm_g[l], 16), ones=ones, rrot=rrot, inv64=inv64)
        ncA = _get_nc('A')
        mapsA = []
        for c in range(8):
            m = dict(common)
            m.update(xh=xhs[c], pos=pos_c[c], kv_a_g=_col(kv_a_norm_g[l], 4), w_ukv=f(w_ukv[l]),
                     kg_n=f(mla_k_norm_g[l][:128]).reshape(128, 1), kg_r=f(mla_k_norm_g[l][128:]).reshape(64, 1))
            mapsA.append(m)
        resA = run_bass_kernel_spmd(ncA, mapsA, core_ids=list(range(8)))
        KTb = np.empty((2, 8, 192, 8192), np.float32)
        Vb = np.empty((2, 8192, 1024), np.float32)
        for c in range(8):
            b, r = c // 4, c % 4
            ko = np.asarray(resA.results[c]["kout"])
            vo = np.asarray(resA.results[c]["vout"])
            for j in range(NCH):
                g = 4 * j + r
                KTb[b, :, :, g * C:(g + 1) * C] = ko[j]
                Vb[b, g * C:(g + 1) * C, :] = vo[j]
        ncB = _get_nc('B')
        mapsB = []
        cw = np.ascontiguousarray(np.asarray(conv_w[l], np.float32).reshape(3, 8, 128).transpose(2, 0, 1))
        for c in range(8):
            b, r = c // 4, c % 4
            m = dict(common)
            m.update(xh=xhs[c], pos=pos_c[c], gate_b=_col(gate_b[l], 64), pool_w=f(pool_w[l]),
                     pool_scale=_col(pool_scale[l], 8), q_a_g=_col(q_a_norm_g[l], 4), w_uq=f(w_uq[l]),
                     qg_n=f(mla_q_norm_g[l][:128]).reshape(128, 1), qg_r=f(mla_q_norm_g[l][128:]).reshape(64, 1),
                     conv_w=cw, memT=np.ascontiguousarray(mem[b].T), mem_g=_col(mem_norm_g[l], 16),
                     w_mem_kv=f(w_mem_kv[l]), xq_g=_col(xattn_q_norm_g[l], 2), xk_g=_col(xattn_k_norm_g[l], 2),
                     w_branch=f(w_branch[l]), w_out=f(w_out[l]), KT=KTb[b], V=Vb[b], masks=masks_r[r],
                     rcnt=np.stack([rc_first if r == 0 else rc_rest, rc_rest]))
            mapsB.append(m)
        resB = run_bass_kernel_spmd(ncB, mapsB, core_ids=list(range(8)))
        xn = np.empty_like(xcur)
        for c in range(8):
            b, r = c // 4, c % 4
            o = np.asarray(resB.results[c]["out"])
            for j in range(NCH):
                g = 4 * j + r
                xn[b, g * C:(g + 1) * C, :] = o[j].T
        xcur = xn
    return xcur.astype(np.float32)
```
